# Optimizing a Trainium2 kernel written in Bass

```python
import jax, jax.numpy as jnp
from jax import lax
import numpy as np

D_MODEL = 1024
BATCH = 16
SEQ = 2048
DEPTH = 1

N_META = 16
D_MIX = D_MODEL
D_CONV = D_MIX // 2
CONV_WIDTH = 31
GLA_HEADS = 4
GLA_DV = (D_MIX - D_CONV) // GLA_HEADS
GLA_DK = GLA_DV // 2
GLA_GATE_RANK = 16
GLA_TAU = 16.0
CHUNK = 64
D_FF = 4 * D_MODEL
LN_EPS = 1e-5
DEEPNORM_ALPHA = (2.0 * DEPTH) ** 0.25
DEEPNORM_BETA = (8.0 * DEPTH) ** -0.25

SPLIT_SIZES = (D_CONV, D_CONV,
               GLA_HEADS * GLA_DK, GLA_HEADS * GLA_DK,
               GLA_HEADS * GLA_DV, GLA_HEADS * GLA_DV,
               GLA_GATE_RANK)
D_IN = sum(SPLIT_SIZES)
SPLIT_IDX = tuple(int(i) for i in np.cumsum(SPLIT_SIZES)[:-1])

kernel_name = "hymba_conformer_gla_deepnorm"


def layer_norm(x, g, b):
    xf = x.astype(jnp.float32)
    mu = jnp.mean(xf, axis=-1, keepdims=True)
    var = jnp.mean(jnp.square(xf - mu), axis=-1, keepdims=True)
    y = (xf - mu) * lax.rsqrt(var + LN_EPS)
    return (y * g.astype(jnp.float32) + b.astype(jnp.float32)).astype(x.dtype)


def rms_norm(x, g):
    xf = x.astype(jnp.float32)
    y = xf * lax.rsqrt(jnp.mean(jnp.square(xf), axis=-1, keepdims=True) + LN_EPS)
    return y * g.astype(jnp.float32)


def conformer_conv(a, gate, conv_w, conv_b, ln_g, ln_b):
    h = a * jax.nn.sigmoid(gate)
    h = lax.conv_general_dilated(
        h, conv_w[:, None, :].astype(h.dtype),
        window_strides=(1,), padding=[(CONV_WIDTH - 1, 0)],
        dimension_numbers=("NWC", "WIO", "NWC"),
        feature_group_count=D_CONV) + conv_b
    return jax.nn.silu(layer_norm(h, ln_g, ln_b))


def gla_chunked(q, k, v, log_g):
    B, T = q.shape[0], q.shape[1]
    pad = (-N_META) % CHUNK
    padw = ((0, 0), (pad, 0), (0, 0), (0, 0))
    q, k, v, log_g = [jnp.pad(t.astype(jnp.float32), padw) for t in (q, k, v, log_g)]
    L = T + pad
    N = L // CHUNK

    def to_chunks(t):
        return t.reshape(B, N, CHUNK, GLA_HEADS, t.shape[-1]).transpose(0, 3, 1, 2, 4)

    q, k, v, log_g = map(to_chunks, (q, k, v, log_g))
    q = q * (GLA_DK ** -0.5)
    b = jnp.cumsum(log_g, axis=3)
    b_last = b[:, :, :, -1:, :]
    qe = q * jnp.exp(b)
    ke = k * jnp.exp(-b)
    kd = k * jnp.exp(b_last - b)

    mask = jnp.tril(jnp.ones((CHUNK, CHUNK), dtype=bool))
    A = jnp.einsum("bhncd,bhnsd->bhncs", qe, ke)
    A = jnp.where(mask, A, 0.0)
    o_intra = jnp.einsum("bhncs,bhnse->bhnce", A, v)

    dS = jnp.einsum("bhncd,bhnce->bhnde", kd, v)
    decay = jnp.exp(b_last[:, :, :, 0, :])

    def step(S, xs):
        dec, upd = xs
        return dec[..., None] * S + upd, S

    S0 = jnp.zeros((B, GLA_HEADS, GLA_DK, GLA_DV), jnp.float32)
    _, S_before = lax.scan(step, S0, (jnp.moveaxis(decay, 2, 0), jnp.moveaxis(dS, 2, 0)))
    S_before = jnp.moveaxis(S_before, 0, 2)
    o_inter = jnp.einsum("bhncd,bhnde->bhnce", qe, S_before)

    o = (o_intra + o_inter).transpose(0, 2, 3, 1, 4).reshape(B, L, GLA_HEADS, GLA_DV)
    return o[:, pad:]


def setup_inputs(seed: int = 0) -> dict:
    key = jax.random.key(seed)
    ks = jax.random.split(key, 20)
    f32 = jnp.float32

    def nrm(k, shape, scale):
        return jax.random.normal(k, shape, f32) * scale

    def gain(k, shape):
        return 1.0 + 0.02 * jax.random.normal(k, shape, f32)

    return {
        "x": jax.random.normal(ks[0], (BATCH, SEQ, D_MODEL), f32),
        "meta_tokens": nrm(ks[1], (N_META, D_MODEL), 1.0),
        "ln_in_g": gain(ks[2], (D_MODEL,)),
        "ln_in_b": nrm(ks[3], (D_MODEL,), 0.02),
        "w_in": nrm(ks[4], (DEPTH, D_MODEL, D_IN), D_MODEL ** -0.5),
        "conv_w": nrm(ks[5], (DEPTH, CONV_WIDTH, D_CONV), CONV_WIDTH ** -0.5),
        "conv_b": nrm(ks[6], (DEPTH, D_CONV), 0.02),
        "conv_ln_g": gain(ks[7], (DEPTH, D_CONV)),
        "conv_ln_b": nrm(ks[8], (DEPTH, D_CONV), 0.02),
        "gate_up": nrm(ks[9], (DEPTH, GLA_GATE_RANK, GLA_HEADS * GLA_DK), GLA_GATE_RANK ** -0.5),
        "gate_bias": nrm(ks[10], (DEPTH, GLA_HEADS * GLA_DK), 0.02),
        "gla_norm_g": gain(ks[11], (DEPTH, GLA_DV)),
        "w_out": nrm(ks[12], (DEPTH, D_MIX, D_MODEL), DEEPNORM_BETA * D_MIX ** -0.5),
        "ln1_g": gain(ks[13], (DEPTH, D_MODEL)),
        "ln1_b": nrm(ks[14], (DEPTH, D_MODEL), 0.02),
        "w_ff1": nrm(ks[15], (DEPTH, D_MODEL, D_FF), D_MODEL ** -0.5),
        "w_ff2": nrm(ks[16], (DEPTH, D_FF, D_MODEL), DEEPNORM_BETA * D_FF ** -0.5),
        "ln2_g": gain(ks[17], (DEPTH, D_MODEL)),
        "ln2_b": nrm(ks[18], (DEPTH, D_MODEL), 0.02),
    }


def reference(x, meta_tokens, ln_in_g, ln_in_b, w_in, conv_w, conv_b, conv_ln_g, conv_ln_b,
              gate_up, gate_bias, gla_norm_g, w_out, ln1_g, ln1_b, w_ff1, w_ff2, ln2_g, ln2_b):
    B = x.shape[0]
    meta = jnp.broadcast_to(meta_tokens.astype(x.dtype)[None], (B, N_META, D_MODEL))
    s = jnp.concatenate([meta, x], axis=1)
    s = layer_norm(s, ln_in_g, ln_in_b)
    T = s.shape[1]

    for l in range(DEPTH):
        u = s @ w_in[l]
        c_val, c_gate, q, k, v, r, g_down = jnp.split(u, SPLIT_IDX, axis=-1)

        conv_out = conformer_conv(c_val, c_gate, conv_w[l], conv_b[l], conv_ln_g[l], conv_ln_b[l])

        z = (g_down @ gate_up[l] + gate_bias[l]).astype(jnp.float32)
        log_g = jax.nn.log_sigmoid(z) / GLA_TAU
        hd = lambda t, d: t.reshape(B, T, GLA_HEADS, d)
        o = gla_chunked(hd(q, GLA_DK), hd(k, GLA_DK), hd(v, GLA_DV), hd(log_g, GLA_DK))
        o = rms_norm(o, gla_norm_g[l]) * jax.nn.silu(hd(r, GLA_DV).astype(jnp.float32))
        gla_out = o.reshape(B, T, GLA_HEADS * GLA_DV).astype(s.dtype)

        mix = jnp.concatenate([conv_out, gla_out], axis=-1) @ w_out[l]
        s = layer_norm(DEEPNORM_ALPHA * s + mix, ln1_g[l], ln1_b[l])

        f = jnp.square(jax.nn.relu(s @ w_ff1[l])) @ w_ff2[l]
        s = layer_norm(DEEPNORM_ALPHA * s + f, ln2_g[l], ln2_b[l])

    return s[:, N_META:]
```

```python
import numpy as np
from contextlib import ExitStack

import concourse.bass as bass
import concourse.mybir as mybir
from concourse.bass_utils import run_bass_kernel_spmd

F32 = mybir.dt.float32
BF16 = mybir.dt.bfloat16
AF = mybir.ActivationFunctionType
ALU = mybir.AluOpType

D = 1024
NMETA = 16
DCONV = 512
CW = 31
DK = 64
DV = 128
NH = 4
RANK = 16
DIN = 2576
DFF = 4096
EPS = 1e-5
ALPHA = 2.0 ** 0.25
T = 512
NS = T // 128
NRING = 2
NSLAB = 12


class Op:
    __slots__ = ("eng", "fn", "R", "W", "chan", "chan_val", "idx", "needs_inc", "inc_val",
                 "deps", "name")


class Sched:
    ENGS = ("pe", "act", "dve", "pool", "sp")

    def __init__(self):
        self.ops = []
        self.chan_count = {}

    def add(self, eng, fn, R=(), W=(), chan=None, name=""):
        op = Op()
        op.eng = eng
        op.fn = fn
        op.R = tuple(R)
        op.W = tuple(W)
        op.chan = chan
        op.chan_val = 0
        if chan is not None:
            self.chan_count[chan] = self.chan_count.get(chan, 0) + 1
            op.chan_val = 16 * self.chan_count[chan]
        op.needs_inc = False
        op.inc_val = 0
        op.deps = []
        op.name = name
        self.ops.append(op)
        return op

    def analyze(self):
        last_w = {}
        readers = {}
        for op in self.ops:
            deps = {}
            for tg in op.R:
                w = last_w.get(tg)
                if w is not None:
                    deps[id(w)] = (w, True)
            for tg in op.W:
                w = last_w.get(tg)
                if w is not None and id(w) not in deps:
                    deps[id(w)] = (w, False)
                for r in readers.get(tg, ()):
                    if id(r) not in deps:
                        deps[id(r)] = (r, False)
            for (d, raw) in deps.values():
                if d is op:
                    continue
                if d.chan is not None:
                    op.deps.append(d)
                elif d.eng == op.eng:
                    if op.chan is not None or (raw and op.eng in ("act", "dve", "pool")):
                        op.deps.append(d)
                        d.needs_inc = True
                else:
                    op.deps.append(d)
                    d.needs_inc = True
            for tg in op.R:
                readers.setdefault(tg, []).append(op)
            for tg in op.W:
                last_w[tg] = op
                readers[tg] = []
        cnt = {e: 0 for e in self.ENGS}
        for op in self.ops:
            if op.needs_inc and op.chan is None:
                cnt[op.eng] += 1
                op.inc_val = cnt[op.eng]

    def emit(self, nc, stack):
        self.analyze()
        sems = {e: stack.enter_context(nc.semaphore("sem_" + e)) for e in self.ENGS}
        csems = {c: stack.enter_context(nc.semaphore("ch_" + c)) for c in self.chan_count}
        block = stack.enter_context(nc.Block())
        by_eng = {e: [o for o in self.ops if o.eng == e] for e in self.ENGS}

        def run(eng_name, eng):
            waited = {}
            for op in by_eng[eng_name]:
                need = {}
                for d in op.deps:
                    if d.chan is not None:
                        key, val = ("c", d.chan), d.chan_val
                        if d.chan == "cst":
                            val = 16 * self.chan_count["cst"]
                    else:
                        key, val = ("e", d.eng), d.inc_val
                    if need.get(key, 0) < val:
                        need[key] = val
                for key, val in need.items():
                    if waited.get(key, 0) >= val:
                        continue
                    waited[key] = val
                    sem = csems[key[1]] if key[0] == "c" else sems[key[1]]
                    eng.wait_ge(sem, val)
                ins = op.fn(eng)
                if op.chan is not None:
                    ins.then_inc(csems[op.chan], 16)
                elif op.needs_inc:
                    ins.then_inc(sems[eng_name], 1)

        @block.tensor
        def _(e):
            run("pe", e)

        @block.scalar
        def _(e):
            run("act", e)

        @block.vector
        def _(e):
            run("dve", e)

        @block.gpsimd
        def _(e):
            run("pool", e)

        @block.sync
        def _(e):
            run("sp", e)
            for c, n in self.chan_count.items():
                e.wait_ge(csems[c], 16 * n)


def tags(name, idxs):
    return [(name, i) for i in idxs]


class _Stop(Exception):
    pass


DBG = {"stop": None}


def build_nc(n_seq, seq_len):
    n_tiles = seq_len // T

    def stop(label):
        if DBG["stop"] == label:
            raise _Stop()
    nc = bass.Bass("TRN2", target_bir_lowering=False)
    S = Sched()
    stack = ExitStack()

    def dram_in(name, shape):
        return nc.dram_tensor(name, list(shape), F32, kind="ExternalInput").ap()

    x_d = dram_in("x", [n_seq, seq_len, D])
    meta_d = dram_in("meta_tokens", [NMETA, D])
    w_in_d = dram_in("w_in", [D, DIN])
    w_out_d = dram_in("w_out", [D, D])
    w_ff1_d = dram_in("w_ff1", [D, DFF])
    w_ff2_d = dram_in("w_ff2", [DFF, D])
    gb_names = ["ln_in_g", "ln_in_b", "ln1_g", "ln1_b", "ln2_g", "ln2_b"]
    gb_d = {n: dram_in(n, [D]) for n in gb_names}
    conv_w_d = dram_in("conv_w", [128, 4, CW])
    conv_b_d = dram_in("conv_b", [128, 4])
    conv_g_d = dram_in("conv_ln_g", [128, 4])
    conv_beta_d = dram_in("conv_ln_b", [128, 4])
    gate_up_d = dram_in("gate_up", [RANK, NH * DK])
    gate_bias_d = dram_in("gate_bias", [NH * DK])
    gnorm_d = dram_in("gla_norm_g", [DV, 1])
    ident_d = dram_in("c_ident", [128, 128])
    umat_d = dram_in("c_umat", [128, 128])
    mask_d = dram_in("c_mask", [128, 128])
    y_d = nc.dram_tensor("y", [n_seq, seq_len, D], F32, kind="ExternalOutput").ap()
    wsc_d = nc.dram_tensor("wsc", [NSLAB, 128, 8 * 1024], BF16).ap()

    def sb(name, shape, dt):
        return stack.enter_context(nc.sbuf_tensor(name, list(shape), dt))

    def ps(name, shape, dt=F32):
        return stack.enter_context(nc.psum_tensor(name, list(shape), dt))

    ring = [sb(f"ring{i}", [128, 8, 1024], BF16) for i in range(NRING)]
    xs = [sb(f"xs{i}", [128, D], F32) for i in range(2)]
    s_tm = sb("s_tm", [128, NS, D], F32)
    s_fm = sb("s_fm", [128, 8, T], BF16)
    outst = [sb(f"outst{i}", [128, D], F32) for i in range(2)]
    gb = {n: sb("bc_" + n, [128, D], F32) for n in gb_names}
    h_fm = sb("h_fm", [128, 4, 30 + T], BF16)
    sig = [sb("sig0", [128, T], F32)] * 2
    qz = sb("qz", [128, 4, T], BF16)
    ke_fm = sb("ke_fm", [128, 2, T], BF16)
    ke_tm = sb("ke_tm", [128, NS, 256], BF16)
    v_tm = sb("v_tm", [128, NS, 512], BF16)
    sr_fm = sb("sr_fm", [128, 4, T], BF16)
    gd_sb = sb("gd_sb", [128, T], BF16)
    zb = sb("zb", [128, 256], F32)
    ez = sb("ez", [128, 256], F32)
    sp_tm = sb("sp_tm", [128, NS, 256], F32)
    eb_tm = [sb(f"eb_tm{i}", [128, 256], F32) for i in range(2)]
    eq_fm = sb("eq_fm", [128, 2, T], F32)
    ek_fm = sb("ek_fm", [128, 2, T], F32)
    at_sb = [sb(f"at_sb{i}", [128, 4, 128], BF16) for i in range(2)]
    o_sb = sb("o_sb", [128, 512], F32)
    osq = sb("osq", [128, 512], BF16)
    rg = sb("rg", [128, 512], F32)
    S_st = sb("S_st", [128, 2, 128], F32)
    S_bf = sb("S_bf", [128, 2, 128], BF16)
    S_tmp = sb("S_tmp", [128, 2, 128], F32)
    S_meta = sb("S_meta", [128, 2, 128], F32)
    h_meta = sb("h_meta", [128, 4, NMETA], BF16)
    mix_fm = sb("mix_fm", [128, 8, T], BF16)
    hc = sb("hc", [128, 4, T], F32)
    hc_bf = sb("hc_bf", [128, 4, T], BF16)
    hsq = sb("hsq", [128, 4, T], BF16)
    mean_sb = sb("mean_sb", [128, T], F32)
    var_sb = sb("var_sb", [128, T], F32)
    h2 = sb("h2", [128, 8, T], BF16)
    r_bf = [sb(f"r_bf{i}", [128, T], BF16) for i in range(2)]
    ident = sb("ident", [128, 128], F32)
    umat = sb("umat", [128, 128], F32)
    maskU = sb("maskU", [128, 128], F32)
    ones_dv = sb("ones_dv", [128, 128], BF16)
    ones_dc = sb("ones_dc", [128, 128], BF16)
    cw_sb = sb("cw_sb", [128, 4, CW], F32)
    cb_sb = sb("cb_sb", [128, 4], F32)
    cg_sb = sb("cg_sb", [128, 4], F32)
    cbeta_sb = sb("cbeta_sb", [128, 4], F32)
    gup_f = sb("gup_f", [RANK, 256], F32)
    gup_bf = sb("gup_bf", [RANK, 256], BF16)
    gbias_bc = sb("gbias_bc", [128, 256], F32)
    gn_sb = sb("gn_sb", [128, 1], F32)
    eps_t = sb("eps_t", [128, 1], F32)
    one_t = sb("one_t", [128, 1], F32)
    stats = sb("stats", [128, 2, 6], F32)
    mv = sb("mv", [128, 2], F32)
    lnv = sb("lnv", [128, 1], F32)
    rstd = sb("rstd", [128, 1], F32)
    lt1 = sb("lt1", [128, D], F32)
    xm_sb = sb("xm_sb", [NMETA, D], F32)
    sm_tm = sb("sm_tm", [NMETA, D], F32)
    sm_fm = sb("sm_fm", [128, 8, NMETA], BF16)

    NG = 5
    pg = [ps(f"pg{i}", [128, 512]) for i in range(NG)]
    p_tp = ps("p_tp", [128, 512])
    p_at = ps("p_at", [128, 512])
    p_o = ps("p_o", [128, 512])
    pg_ctr = [0]

    def next_pg():
        i = pg_ctr[0] % NG
        pg_ctr[0] += 1
        return pg[i], ("pg", i)

    def dma(eng, out, in_, R, W, chan, name=""):
        S.add(eng, lambda e: e.dma_start(out=out, in_=in_), R=R, W=W, chan=chan, name=name)

    def mm_group(out, pairs, R, W, name=""):
        n = len(pairs)

        def fn(e):
            ins = None
            for i, (l, r) in enumerate(pairs):
                ins = e.matmul(out, l, r, start=(i == 0), stop=(i == n - 1))
            return ins
        S.add("pe", fn, R=R, W=W, name=name)

    cc = [0]

    def cload(out, in_, wtag):
        cc[0] += 1
        dma("sp", out, in_, R=[], W=[wtag], chan="cst")

    cload(ident[:, :], ident_d[:, :], "ident")
    cload(umat[:, :], umat_d[:, :], "umat")
    cload(maskU[:, :], mask_d[:, :], "maskU")
    for n in gb_names:
        cload(gb[n][:, :], gb_d[n].partition_broadcast(128), "bc_" + n)
    cload(cw_sb[:, :, :], conv_w_d[:, :, :], "cw")
    cload(cb_sb[:, :], conv_b_d[:, :], "cb")
    cload(cg_sb[:, :], conv_g_d[:, :], "cg")
    cload(cbeta_sb[:, :], conv_beta_d[:, :], "cbeta")
    cload(gup_f[:, :], gate_up_d[:, :], "gup_f")
    cload(gbias_bc[:, :], gate_bias_d.partition_broadcast(128), "gbias")
    cload(gn_sb[:, :], gnorm_d[:, :], "gn")
    cload(xm_sb[:, :], meta_d[:, :], "xm")
    S.add("pool", lambda e: e.memset(ones_dv[:, :], 1.0 / DV), W=["ones_dv"])
    S.add("pool", lambda e: e.memset(ones_dc[:, :], 1.0 / DCONV), W=["ones_dc"])
    S.add("pool", lambda e: e.memset(eps_t[:, :], EPS), W=["eps"])
    S.add("pool", lambda e: e.memset(one_t[:, :], 1.0), W=["one"])
    S.add("pool", lambda e: e.memset(qz[:, :, :], 0.0), W=tags("qz", range(4)))
    S.add("pool", lambda e: e.tensor_copy(gup_bf[:, :], gup_f[:, :]), R=["gup_f"], W=["gup"])

    def slab_src(s):
        if s == 0:
            return w_in_d[:, 2048:2576].rearrange("(k p) c -> p k c", p=128), 528
        if s == 1:
            return w_in_d[:, 1024:2048].rearrange("(k p) c -> p k c", p=128), 1024
        if s == 2:
            return w_in_d[:, 0:1024].rearrange("(k p) c -> p k c", p=128), 1024
        if s == 3:
            return w_out_d[:, :].rearrange("(k p) c -> p k c", p=128), 1024
        g = (s - 4) // 2
        if (s - 4) % 2 == 0:
            return w_ff1_d[:, g * 1024:(g + 1) * 1024].rearrange("(k p) c -> p k c", p=128), 1024
        return w_ff2_d[g * 1024:(g + 1) * 1024, :].rearrange("(k p) c -> p k c", p=128), 1024

    slab_ctr = [0]

    def load_slab(s, first):
        i = slab_ctr[0] % NRING
        slab_ctr[0] += 1
        rt = ring[i]
        tg = ("ring", i)
        if first:
            src, ncol = slab_src(s)
            dma("pool", rt[:, :, 0:ncol], src, R=[], W=[tg], chan=f"ringsw{i}", name=f"ldslab{s}")
            dma("sp", wsc_d[s].rearrange("p (k c) -> p k c", k=8)[:, :, 0:ncol], rt[:, :, 0:ncol],
                R=[tg], W=[("wsc", s)], chan=f"wsc{i}", name=f"stslab{s}")
        else:
            ncol = 528 if s == 0 else 1024
            dma("sp", rt[:, :, 0:ncol], wsc_d[s].rearrange("p (k c) -> p k c", k=8)[:, :, 0:ncol],
                R=[("wsc", s)], W=[tg], chan=f"ring{i}", name=f"ldslab{s}")
        return rt, tg

    def ln_tm(src, src_tags, n_tok, gname, bname, dst, dst_tags, src_is_psum=False):
        P = slice(0, n_tok)
        for hh in range(2):
            S.add("dve", lambda e, hh=hh: e.bn_stats(stats[P, hh, :], src[:, hh * 512:(hh + 1) * 512]),
                  R=src_tags, W=[("stats", hh)])
        S.add("dve", lambda e: e.bn_aggr(mv[P, :], stats[P, :, :]), R=tags("stats", range(2)), W=["mv"])
        S.add("act", lambda e: e.activation(lnv[P, :], mv[P, 1:2], AF.Ln, bias=eps_t[P, :]),
              R=["mv", "eps"], W=["lnv"])
        S.add("act", lambda e: e.activation(rstd[P, :], lnv[P, :], AF.Exp, scale=-0.5),
              R=["lnv"], W=["rstd"])
        S.add("dve", lambda e: e.scalar_tensor_tensor(lt1[P, :], src, mv[P, 0:1], gb[gname][P, :],
                                                      ALU.subtract, ALU.mult),
              R=list(src_tags) + ["mv", "bc_" + gname], W=["lt1"])
        S.add("dve", lambda e: e.scalar_tensor_tensor(dst, lt1[P, :], rstd[P, :], gb[bname][P, :],
                                                      ALU.mult, ALU.add),
              R=["lt1", "rstd", "bc_" + bname], W=dst_tags)

    def to_fm(src_tm, src_tags, n_tok, dst_fm, dst_tags):
        for half in range(2):
            def fn(e, half=half):
                ins = None
                for k in range(4):
                    kc = half * 4 + k
                    ins = e.transpose(p_tp[:, k * 128:k * 128 + n_tok],
                                      src_tm[:, kc * 128:(kc + 1) * 128], ident[0:n_tok, 0:n_tok])
                return ins
            S.add("pe", fn, R=list(src_tags) + ["ident"], W=["p_tp"])
            S.add("act", lambda e, half=half: e.activation(
                dst_fm[:, half * 4:(half + 1) * 4, :],
                p_tp[:, :].rearrange("p (k c) -> p k c", k=4)[:, :, 0:n_tok], AF.Copy),
                R=["p_tp"], W=dst_tags)

    def gla_gates(gd_cols, L, sp_dst, sp_tag):
        P = slice(0, L)
        pz, pzt = next_pg()
        mm_group(pz[P, 0:256], [(gd_cols, gup_bf[:, :])], R=["gd", "gup"], W=[pzt])
        S.add("dve", lambda e: e.tensor_tensor(zb[P, :], pz[P, 0:256], gbias_bc[P, :], ALU.add),
              R=[pzt, "gbias"], W=["zb"])
        S.add("act", lambda e: e.activation(ez[P, :], zb[P, :], AF.Exp, scale=-1.0), R=["zb"], W=["ez"])
        S.add("act", lambda e: e.activation(sp_dst, ez[P, :], AF.Ln, bias=one_t[P, :]),
              R=["ez", "one"], W=[sp_tag])

    def state_update(ke_ap, ke_tag, v_ap, v_tag, dec_aps, dec_tags, L):
        for pr in range(2):
            pd, pdt = next_pg()
            mm_group(pd[:, 0:256], [(ke_ap[:, pr * 128:(pr + 1) * 128], v_ap[:, pr * 256:(pr + 1) * 256])],
                     R=[ke_tag, v_tag], W=[pdt])
            for hh in range(2):
                Ph = slice(hh * 64, (hh + 1) * 64)
                S.add("dve", lambda e, pr=pr, hh=hh, Ph=Ph, pd=pd: e.tensor_tensor(
                    S_tmp[Ph, pr, :], pd[Ph, hh * 128:(hh + 1) * 128], S_st[Ph, pr, :], ALU.add),
                    R=[pdt, ("S_st", pr)], W=[("S_tmp", pr, hh)])
            S.add("dve", lambda e, pr=pr: e.tensor_scalar(S_st[:, pr, :], S_tmp[:, pr, :], dec_aps[pr], None,
                                                          ALU.mult),
                  R=[("S_tmp", pr, 0), ("S_tmp", pr, 1), dec_tags[pr]], W=[("S_st", pr)])
            S.add("act", lambda e, pr=pr: e.activation(S_bf[:, pr, :], S_st[:, pr, :], AF.Copy),
                  R=[("S_st", pr)], W=[("S_bf", pr)])

    def meta_init(slabs):
        PM = slice(0, NMETA)
        ln_tm(xm_sb[:, :], ["xm"], NMETA, "ln_in_g", "ln_in_b", sm_tm[:, :], ["sm_tm"])
        to_fm(sm_tm[:, :], ["sm_tm"], NMETA, sm_fm, ["sm_fm"])


    xs_ctr = [0]
    out_ctr = [0]
    first_tile = True
    try:
      for q in range(n_seq):
        for ti in range(n_tiles):
            t0 = ti * T
            do_meta = first_tile
            if do_meta:
                meta_init(None)
            for t in range(NS):
                b = xs_ctr[0] % 2
                xs_ctr[0] += 1
                dma("sp", xs[b][:, :], x_d[q, t0 + t * 128:t0 + (t + 1) * 128, :], R=[], W=[("xs", b)],
                    chan=f"xs{b}")
                ln_tm(xs[b][:, :], [("xs", b)], 128, "ln_in_g", "ln_in_b", s_tm[:, t, :], [("s_tm", t)])
                to_fm(s_tm[:, t, :], [("s_tm", t)], 128, s_fm[:, :, t * 128:(t + 1) * 128], [("s_fm", t)])
            sfm_tags = tags("s_fm", range(NS))
            stop("A")

            rt, rtg = load_slab(0, first_tile)
            pgd, pgdt = next_pg()
            mm_group(pgd[0:RANK, :], [(rt[:, kc, 512:528], s_fm[:, kc, :]) for kc in range(8)],
                     R=[rtg] + sfm_tags, W=[pgdt])
            S.add("act", lambda e, pgd=pgd: e.activation(gd_sb[0:RANK, :], pgd[0:RANK, :], AF.Copy),
                  R=[pgdt], W=["gd"])
            if do_meta:
                pgm, pgmt = next_pg()
                mm_group(pgm[0:RANK, 0:NMETA], [(rt[:, kc, 512:528], sm_fm[:, kc, :]) for kc in range(8)],
                         R=[rtg, "sm_fm"], W=[pgmt])
                gdm = sb("gdm", [RANK, NMETA], BF16)
                S.add("act", lambda e, pgm=pgm: e.activation(gdm[:, :], pgm[0:RANK, 0:NMETA], AF.Copy),
                      R=[pgmt], W=["gdm"])
            for hh in range(4):
                pr_, prt = next_pg()
                mm_group(pr_[:, :], [(rt[:, kc, hh * 128:(hh + 1) * 128], s_fm[:, kc, :]) for kc in range(8)],
                         R=[rtg] + sfm_tags, W=[prt])
                S.add("act", lambda e, pr_=pr_, hh=hh: e.activation(sr_fm[:, hh, :], pr_[:, :], AF.Silu),
                      R=[prt], W=[("sr_fm", hh)])

            stop("B0")
            if do_meta:
                spm = sb("spm", [NMETA, 256], F32)
                PM = slice(0, NMETA)
                pz, pzt = next_pg()
                mm_group(pz[PM, 0:256], [(gdm[:, :], gup_bf[:, :])], R=["gdm", "gup"], W=[pzt])
                S.add("dve", lambda e, pz=pz: e.tensor_tensor(zb[PM, :], pz[PM, 0:256], gbias_bc[PM, :], ALU.add),
                      R=[pzt, "gbias"], W=["zb"])
                S.add("act", lambda e: e.activation(ez[PM, :], zb[PM, :], AF.Exp, scale=-1.0), R=["zb"], W=["ez"])
                S.add("act", lambda e: e.activation(spm[:, :], ez[PM, :], AF.Ln, bias=one_t[PM, :]),
                      R=["ez", "one"], W=["spm"])
                pbm, pbmt = next_pg()
                mm_group(pbm[PM, 0:256], [(umat[PM, 0:NMETA], spm[:, :])], R=["umat", "spm"], W=[pbmt])
                ebm = sb("ebm", [NMETA, 256], F32)
                S.add("act", lambda e, pbm=pbm: e.activation(ebm[:, :], pbm[PM, 0:256], AF.Exp, scale=-1.0),
                      R=[pbmt], W=["ebm"])
                pfm, pfmt = next_pg()
                for pr in range(2):
                    mm_group(pfm[:, pr * 128:pr * 128 + NMETA],
                             [(spm[:, pr * 128:(pr + 1) * 128], umat[PM, 0:NMETA])],
                             R=["umat", "spm"], W=[pfmt])
                decm = sb("decm", [128, 2], F32)
                S.add("act", lambda e, pfm=pfm: e.activation(
                    decm[:, :], pfm[:, :].rearrange("p (a c) -> p a c", a=4)[:, 0:2, NMETA - 1], AF.Exp),
                    R=[pfmt], W=["decm"])
            for t in range(NS):
                PA = slice(0, 128)
                pz, pzt = next_pg()
                mm_group(pz[:, 0:256], [(gd_sb[0:RANK, t * 128:(t + 1) * 128], gup_bf[:, :])],
                         R=["gd", "gup"], W=[pzt])
                S.add("dve", lambda e, pz=pz: e.tensor_tensor(zb[:, :], pz[:, 0:256], gbias_bc[:, :], ALU.add),
                      R=[pzt, "gbias"], W=["zb"])
                S.add("act", lambda e: e.activation(ez[:, :], zb[:, :], AF.Exp, scale=-1.0), R=["zb"], W=["ez"])
                S.add("act", lambda e, t=t: e.activation(sp_tm[:, t, :], ez[:, :], AF.Ln, bias=one_t[:, :]),
                      R=["ez", "one"], W=[("sp_tm", t)])
                pf, pft = next_pg()
                for pr in range(2):
                    mm_group(pf[:, pr * 128:(pr + 1) * 128],
                             [(sp_tm[:, t, pr * 128:(pr + 1) * 128], umat[:, :])],
                             R=["umat", ("sp_tm", t)], W=[pft])
                cs = slice(t * 128, (t + 1) * 128)
                S.add("act", lambda e, pf=pf, cs=cs: e.activation(
                    eq_fm[:, :, cs], pf[:, 0:256].rearrange("p (a c) -> p a c", a=2), AF.Exp),
                    R=[pft], W=[("eq_fm", t)])
                S.add("act", lambda e, pf=pf, cs=cs: e.activation(
                    ek_fm[:, :, cs], pf[:, 0:256].rearrange("p (a c) -> p a c", a=2), AF.Exp, scale=-1.0),
                    R=[pft], W=[("ek_fm", t)])

            stop("B1")
            rt, rtg = load_slab(1, first_tile)
            if do_meta:
                PM = slice(0, NMETA)
                pkm, pkmt = next_pg()
                mm_group(pkm[PM, 0:256], [(sm_fm[:, kc, :], rt[:, kc, 256:512]) for kc in range(8)],
                         R=[rtg, "sm_fm"], W=[pkmt])
                kem = sb("kem", [NMETA, 256], BF16)
                S.add("dve", lambda e, pkm=pkm: e.tensor_tensor(kem[:, :], pkm[PM, 0:256], ebm[:, :], ALU.mult),
                      R=[pkmt, "ebm"], W=["kem"])
                pvm, pvmt = next_pg()
                mm_group(pvm[PM, :], [(sm_fm[:, kc, :], rt[:, kc, 512:1024]) for kc in range(8)],
                         R=[rtg, "sm_fm"], W=[pvmt])
                vm = sb("vm", [NMETA, 512], BF16)
                S.add("act", lambda e, pvm=pvm: e.activation(vm[:, :], pvm[PM, :], AF.Copy), R=[pvmt], W=["vm"])
            for pr in range(2):
                pq, pqt = next_pg()
                mm_group(pq[:, :], [(rt[:, kc, pr * 128:(pr + 1) * 128], s_fm[:, kc, :]) for kc in range(8)],
                         R=[rtg] + sfm_tags, W=[pqt])
                for hh in range(2):
                    Ph = slice(hh * 64, (hh + 1) * 64)
                    S.add("dve", lambda e, pq=pq, pr=pr, hh=hh, Ph=Ph: e.scalar_tensor_tensor(
                        qz[Ph, 2 * pr + hh, :], pq[Ph, :], DK ** -0.5, eq_fm[Ph, pr, :], ALU.mult, ALU.mult),
                        R=[pqt] + tags("eq_fm", range(NS)), W=[("qz", 2 * pr + hh)])
                pk, pkt = next_pg()
                mm_group(pk[:, :], [(rt[:, kc, 256 + pr * 128:256 + (pr + 1) * 128], s_fm[:, kc, :])
                                    for kc in range(8)],
                         R=[rtg] + sfm_tags, W=[pkt])
                S.add("dve", lambda e, pk=pk, pr=pr: e.tensor_tensor(
                    ke_fm[:, pr, :], pk[:, :], ek_fm[:, pr, :], ALU.mult),
                    R=[pkt] + tags("ek_fm", range(NS)), W=[("ke_fm", pr)])
            for t in range(NS):
                cs = slice(t * 128, (t + 1) * 128)
                pb, pbt = next_pg()
                mm_group(pb[:, 0:256], [(umat[:, :], sp_tm[:, t, :])], R=["umat", ("sp_tm", t)], W=[pbt])
                eb = eb_tm[t % 2]
                ebt = ("eb_tm", t % 2)
                S.add("act", lambda e, pb=pb, eb=eb: e.activation(eb[:, :], pb[:, 0:256], AF.Exp, scale=-1.0),
                      R=[pbt], W=[ebt])
                pk, pkt = next_pg()
                mm_group(pk[:, 0:256], [(s_fm[:, kc, cs], rt[:, kc, 256:512]) for kc in range(8)],
                         R=[rtg, ("s_fm", t)], W=[pkt])
                S.add("dve", lambda e, pk=pk, eb=eb, t=t: e.tensor_tensor(
                    ke_tm[:, t, :], pk[:, 0:256], eb[:, :], ALU.mult),
                    R=[pkt, ebt], W=[("ke_tm", t)])
                pv, pvt = next_pg()
                mm_group(pv[:, :], [(s_fm[:, kc, cs], rt[:, kc, 512:1024]) for kc in range(8)],
                         R=[rtg, ("s_fm", t)], W=[pvt])
                S.add("act", lambda e, pv=pv, t=t: e.activation(v_tm[:, t, :], pv[:, :], AF.Copy),
                      R=[pvt], W=[("v_tm", t)])

            stop("B2")
            rt, rtg = load_slab(2, first_tile)
            if ti == 0:
                if do_meta:
                    for c4 in range(4):
                        pvl, pvlt = next_pg()
                        mm_group(pvl[:, 0:NMETA], [(rt[:, kc, c4 * 128:(c4 + 1) * 128], sm_fm[:, kc, :])
                                                   for kc in range(8)], R=[rtg, "sm_fm"], W=[pvlt])
                        pgt, pgtt = next_pg()
                        mm_group(pgt[:, 0:NMETA], [(rt[:, kc, 512 + c4 * 128:512 + (c4 + 1) * 128], sm_fm[:, kc, :])
                                                   for kc in range(8)], R=[rtg, "sm_fm"], W=[pgtt])
                        S.add("act", lambda e, pgt=pgt: e.activation(sig[0][:, 0:NMETA], pgt[:, 0:NMETA], AF.Sigmoid),
                              R=[pgtt], W=[("sig", 0)])
                        S.add("dve", lambda e, pvl=pvl, c4=c4: e.tensor_tensor(
                            h_meta[:, c4, :], pvl[:, 0:NMETA], sig[0][:, 0:NMETA], ALU.mult),
                            R=[pvlt, ("sig", 0)], W=[("h_meta", c4)])
                S.add("pool", lambda e: e.memset(h_fm[:, :, 0:14], 0.0), W=tags("h_hist", range(4)))
                S.add("pool", lambda e: e.tensor_copy(h_fm[:, :, 14:30], h_meta[:, :, :]),
                      R=tags("h_meta", range(4)), W=tags("h_hist", range(4)))
            else:
                for c4 in range(4):
                    S.add("pool", lambda e, c4=c4: e.tensor_copy(h_fm[:, c4, 0:30], h_fm[:, c4, T:T + 30]),
                          R=[("h_cur", c4)], W=[("h_hist", c4)])
            for c4 in range(4):
                pvl, pvlt = next_pg()
                mm_group(pvl[:, :], [(rt[:, kc, c4 * 128:(c4 + 1) * 128], s_fm[:, kc, :]) for kc in range(8)],
                         R=[rtg] + sfm_tags, W=[pvlt])
                pgt, pgtt = next_pg()
                mm_group(pgt[:, :], [(rt[:, kc, 512 + c4 * 128:512 + (c4 + 1) * 128], s_fm[:, kc, :])
                                     for kc in range(8)], R=[rtg] + sfm_tags, W=[pgtt])
                sg = sig[c4 % 2]
                sgt = ("sig", 0)
                S.add("act", lambda e, pgt=pgt, sg=sg: e.activation(sg[:, :], pgt[:, :], AF.Sigmoid),
                      R=[pgtt], W=[sgt])
                S.add("dve", lambda e, pvl=pvl, sg=sg, c4=c4: e.tensor_tensor(
                    h_fm[:, c4, 30:30 + T], pvl[:, :], sg[:, :], ALU.mult),
                    R=[pvlt, sgt, ("h_hist", c4)], W=[("h_cur", c4)])

            stop("C")
            if ti == 0:
                if do_meta:
                    S.add("pool", lambda e: e.memset(S_st[:, :, :], 0.0), W=tags("S_st", range(2)))
                    state_update(kem, "kem", vm, "vm", [decm[:, 0:1], decm[:, 1:2]], ["decm", "decm"], NMETA)
                    S.add("pool", lambda e: e.tensor_copy(S_meta[:, :, :], S_st[:, :, :]),
                          R=tags("S_st", range(2)), W=["S_meta"])
                else:
                    S.add("pool", lambda e: e.tensor_copy(S_st[:, :, :], S_meta[:, :, :]),
                          R=["S_meta"], W=tags("S_st", range(2)))
                    S.add("pool", lambda e: e.tensor_copy(S_bf[:, :, :], S_meta[:, :, :]),
                          R=["S_meta"], W=tags("S_bf", range(2)))

            for t in range(NS):
                cs = slice(t * 128, (t + 1) * 128)
                atb = at_sb[t % 2]
                att = ("at_sb", t % 2)
                for h in range(NH):
                    pr, hh = h // 2, h % 2
                    Ph = slice(hh * 64, (hh + 1) * 64)
                    mm_group(p_at[:, h * 128:(h + 1) * 128], [(ke_fm[:, pr, cs], qz[:, h, cs])],
                             R=[("ke_fm", pr), ("qz", h)], W=["p_at"])
                S.add("dve", lambda e, atb=atb: e.tensor_tensor(
                    atb[:, :, :], p_at[:, :].rearrange("p (h c) -> p h c", h=4),
                    maskU[:, :].unsqueeze(1).to_broadcast([128, 4, 128]), ALU.mult),
                    R=["p_at", "maskU"], W=[att])
                stop("D1")
                for h in range(NH):
                    pr, hh = h // 2, h % 2
                    Ph = slice(hh * 64, (hh + 1) * 64)
                    mm_group(p_o[:, h * 128:(h + 1) * 128],
                             [(v_tm[:, t, h * 128:(h + 1) * 128], atb[:, h, :]),
                              (S_bf[:, pr, :], qz[:, h, cs])],
                             R=[("v_tm", t), att, ("S_bf", pr), ("qz", h)],
                             W=["p_o"])
                stop("D2")
                S.add("act", lambda e: e.activation(o_sb[:, :], p_o[:, :], AF.Copy), R=["p_o"], W=["o_sb", "t1"])
                S.add("act", lambda e: e.activation(osq[:, :], p_o[:, :], AF.Square), R=["p_o"], W=["osq"])
                pss, psst = next_pg()
                mm_group(pss[:, :], [(ones_dv[:, :], osq[:, :])], R=["ones_dv", "osq"], W=[psst])
                S.add("act", lambda e, pss=pss: e.activation(rg[:, :], pss[:, :], AF.Ln, bias=eps_t[:, :]),
                      R=[psst, "eps"], W=["rg0"])
                S.add("act", lambda e: e.activation(rg[:, :], rg[:, :], AF.Exp, scale=-0.5), R=["rg0"], W=["rg"])
                S.add("dve", lambda e: e.tensor_tensor(o_sb[:, :], o_sb[:, :], rg[:, :], ALU.mult),
                      R=["o_sb", "rg"], W=["t1"])
                S.add("dve", lambda e, cs=cs: e.scalar_tensor_tensor(
                    mix_fm[:, 4:8, cs], o_sb[:, :].rearrange("p (h c) -> p h c", h=4), gn_sb[:, 0:1],
                    sr_fm[:, :, cs], ALU.mult, ALU.mult),
                    R=["t1", "gn"] + tags("sr_fm", range(4)), W=[("mix_g", t)])
                stop("D3")
                state_update(ke_tm[:, t, :], ("ke_tm", t), v_tm[:, t, :], ("v_tm", t),
                             [eq_fm[:, 0, t * 128 + 127:t * 128 + 128], eq_fm[:, 1, t * 128 + 127:t * 128 + 128]],
                             [("eq_fm", t), ("eq_fm", t)], 128)

            stop("E")
            for c4 in range(4):
                eng = "dve"
                hr = [("h_cur", c4), ("h_hist", c4), "cw", "cb"]
                S.add(eng, lambda e, c4=c4: e.tensor_scalar(
                    hc[:, c4, :], h_fm[:, c4, 0:T], cw_sb[:, c4, 0:1], cb_sb[:, c4:c4 + 1], ALU.mult, ALU.add),
                    R=hr, W=[("hc", c4)])
                for j in range(1, CW):
                    S.add(eng, lambda e, c4=c4, j=j: e.scalar_tensor_tensor(
                        hc[:, c4, :], h_fm[:, c4, j:j + T], cw_sb[:, c4, j:j + 1], hc[:, c4, :],
                        ALU.mult, ALU.add), R=hr + [("hc", c4)], W=[("hc", c4)])
                S.add("act", lambda e, c4=c4: e.activation(hc_bf[:, c4, :], hc[:, c4, :], AF.Copy),
                      R=[("hc", c4)], W=[("hc_bf", c4)])
                S.add("act", lambda e, c4=c4: e.activation(hsq[:, c4, :], hc[:, c4, :], AF.Square),
                      R=[("hc", c4)], W=[("hsq", c4)])
            pmean, pmeant = next_pg()
            mm_group(pmean[:, :], [(ones_dc[:, :], hc_bf[:, c4, :]) for c4 in range(4)],
                     R=["ones_dc"] + tags("hc_bf", range(4)), W=[pmeant])
            pmsq, pmsqt = next_pg()
            mm_group(pmsq[:, :], [(ones_dc[:, :], hsq[:, c4, :]) for c4 in range(4)],
                     R=["ones_dc"] + tags("hsq", range(4)), W=[pmsqt])
            S.add("act", lambda e, pmean=pmean: e.activation(mean_sb[:, :], pmean[:, :], AF.Copy),
                  R=[pmeant], W=["mean_sb"])
            S.add("act", lambda e, pmean=pmean: e.activation(var_sb[:, :], pmean[:, :], AF.Square),
                  R=[pmeant], W=["m2"])
            S.add("dve", lambda e, pmsq=pmsq: e.tensor_tensor(var_sb[:, :], pmsq[:, :], var_sb[:, :], ALU.subtract),
                  R=[pmsqt, "m2"], W=["var"])
            S.add("act", lambda e: e.activation(var_sb[:, :], var_sb[:, :], AF.Ln, bias=eps_t[:, :]),
                  R=["var", "eps"], W=["lnvar"])
            S.add("act", lambda e: e.activation(var_sb[:, :], var_sb[:, :], AF.Exp, scale=-0.5),
                  R=["lnvar"], W=["rstd_c"])
            for c4 in range(4):
                S.add("pool", lambda e, c4=c4: e.tensor_tensor(hc[:, c4, :], hc[:, c4, :], mean_sb[:, :], ALU.subtract),
                      R=[("hc", c4), "mean_sb", ("hc_bf", c4), ("hsq", c4)], W=[("hcd", c4)])
                S.add("pool", lambda e, c4=c4: e.tensor_tensor(hc[:, c4, :], hc[:, c4, :], var_sb[:, :], ALU.mult),
                      R=[("hcd", c4), "rstd_c"], W=[("hcn", c4)])
                S.add("act", lambda e, c4=c4: e.activation(
                    mix_fm[:, c4, :], hc[:, c4, :], AF.Silu, bias=cbeta_sb[:, c4:c4 + 1], scale=cg_sb[:, c4:c4 + 1]),
                    R=[("hcn", c4), "cg", "cbeta"], W=[("mix_c", c4)])

            stop("F")
            rt, rtg = load_slab(3, first_tile)
            mix_tags = tags("mix_c", range(4)) + tags("mix_g", range(NS))
            for t in range(NS):
                cs = slice(t * 128, (t + 1) * 128)
                for half in range(2):
                    po, pot = next_pg()
                    mm_group(po[:, :], [(mix_fm[:, kc, cs], rt[:, kc, half * 512:(half + 1) * 512])
                                        for kc in range(8)], R=[rtg] + mix_tags, W=[pot])
                    S.add("dve", lambda e, po=po, t=t, half=half: e.scalar_tensor_tensor(
                        s_tm[:, t, half * 512:(half + 1) * 512], s_tm[:, t, half * 512:(half + 1) * 512],
                        ALPHA, po[:, :], ALU.mult, ALU.add),
                        R=[pot, ("s_tm", t)], W=[("s_tm", t)])
                ln_tm(s_tm[:, t, :], [("s_tm", t)], 128, "ln1_g", "ln1_b", s_tm[:, t, :], [("s_tm", t)])
                to_fm(s_tm[:, t, :], [("s_tm", t)], 128, s_fm[:, :, cs], [("s_fm", t)])

            stop("G")
            for g in range(4):
                rta, rtga = load_slab(4 + 2 * g, first_tile)
                rtb, rtgb = load_slab(5 + 2 * g, first_tile)
                for jj in range(8):
                    pf1, pf1t = next_pg()
                    mm_group(pf1[:, :], [(rta[:, kc, jj * 128:(jj + 1) * 128], s_fm[:, kc, :]) for kc in range(8)],
                             R=[rtga] + sfm_tags, W=[pf1t])
                    rb = r_bf[jj % 2]
                    rbt = ("r_bf", jj % 2)
                    S.add("act", lambda e, pf1=pf1, rb=rb: e.activation(rb[:, :], pf1[:, :], AF.Relu),
                          R=[pf1t], W=[rbt])
                    S.add("act", lambda e, rb=rb, jj=jj: e.activation(h2[:, jj, :], rb[:, :], AF.Square),
                          R=[rbt], W=[("h2", jj)])
                for t in range(NS):
                    cs = slice(t * 128, (t + 1) * 128)
                    for half in range(2):
                        pf2, pf2t = next_pg()
                        mm_group(pf2[:, :], [(h2[:, jj, cs], rtb[:, jj, half * 512:(half + 1) * 512])
                                             for jj in range(8)], R=[rtgb] + tags("h2", range(8)), W=[pf2t])
                        dst = s_tm[:, t, half * 512:(half + 1) * 512]
                        if g == 0:
                            S.add("dve", lambda e, pf2=pf2, dst=dst: e.scalar_tensor_tensor(
                                dst, dst, ALPHA, pf2[:, :], ALU.mult, ALU.add),
                                R=[pf2t, ("s_tm", t)], W=[("s_tm", t)])
                        else:
                            S.add("dve", lambda e, pf2=pf2, dst=dst: e.tensor_tensor(dst, dst, pf2[:, :], ALU.add),
                                  R=[pf2t, ("s_tm", t)], W=[("s_tm", t)])

            stop("H")
            for t in range(NS):
                b = out_ctr[0] % 2
                out_ctr[0] += 1
                ln_tm(s_tm[:, t, :], [("s_tm", t)], 128, "ln2_g", "ln2_b", outst[b][:, :], [("outst", b)])
                dma("sp", y_d[q, t0 + t * 128:t0 + (t + 1) * 128, :], outst[b][:, :], R=[("outst", b)],
                    W=[("y", q, ti, t)], chan=f"out{b}")
            first_tile = False
    except _Stop:
        pass

    S.emit(nc, stack)
    stack.close()
    return nc


def _consts():
    s = np.arange(128)[:, None]
    c = np.arange(128)[None, :]
    le = (s <= c).astype(np.float32)
    return {
        "c_ident": np.eye(128, dtype=np.float32),
        "c_umat": (le * (-1.0 / 16.0)).astype(np.float32),
        "c_mask": le,
    }


def run(inputs, n_cores, n_seq, seq_len):
    nc = build_nc(n_seq, seq_len)
    f = lambda a: np.ascontiguousarray(np.asarray(a, dtype=np.float32))
    shared = {
        "meta_tokens": f(inputs["meta_tokens"]),
        "w_in": f(inputs["w_in"][0]), "w_out": f(inputs["w_out"][0]),
        "w_ff1": f(inputs["w_ff1"][0]), "w_ff2": f(inputs["w_ff2"][0]),
        "ln_in_g": f(inputs["ln_in_g"]), "ln_in_b": f(inputs["ln_in_b"]),
        "ln1_g": f(inputs["ln1_g"][0]), "ln1_b": f(inputs["ln1_b"][0]),
        "ln2_g": f(inputs["ln2_g"][0]), "ln2_b": f(inputs["ln2_b"][0]),
        "conv_w": f(np.asarray(inputs["conv_w"][0]).T.reshape(4, 128, CW).transpose(1, 0, 2)),
        "conv_b": f(np.asarray(inputs["conv_b"][0]).reshape(4, 128).T),
        "conv_ln_g": f(np.asarray(inputs["conv_ln_g"][0]).reshape(4, 128).T),
        "conv_ln_b": f(np.asarray(inputs["conv_ln_b"][0]).reshape(4, 128).T),
        "gate_up": f(inputs["gate_up"][0]), "gate_bias": f(inputs["gate_bias"][0]),
        "gla_norm_g": f(np.asarray(inputs["gla_norm_g"][0]).reshape(DV, 1)),
    }
    shared.update(_consts())
    x = f(inputs["x"])
    in_maps = []
    for c in range(n_cores):
        m = dict(shared)
        m["x"] = np.ascontiguousarray(x[c * n_seq:(c + 1) * n_seq])
        in_maps.append(m)
    res = run_bass_kernel_spmd(nc, in_maps, core_ids=list(range(n_cores)))
    return np.concatenate([np.asarray(r["y"], dtype=np.float32) for r in res.results], axis=0)


def kernel(**inputs):
    return run(inputs, 8, 2, 2048)
```
